# Optimizing a Trainium2 kernel written in Bass

```python
import math
import jax, jax.numpy as jnp
from jax import lax
import numpy as np

D_MODEL = 2048
BATCH = 2
SEQ = 4096
DEPTH = 1

N_META = 16
EPS = 1e-6
MLA_HEADS = 16
Q_LORA = 512
KV_LORA = 512
QK_NOPE = 128
QK_ROPE = 64
V_HEAD = 128
ROPE_THETA = 10000.0
Q_BLOCK = 128
SSM_WIDTH = 1024
SSM_GROUP = 16
SSM_GROUPS = SSM_WIDTH // SSM_GROUP
SSM_STATE = 64
DT_MIN = 1e-3
DT_MAX = 1e-1
N_BRANCH = 2
D_FF = -(-8 * D_MODEL // (3 * 256)) * 256
OFF_Q = Q_LORA
OFF_KV = OFF_Q + KV_LORA
OFF_KR = OFF_KV + QK_ROPE
OFF_U = OFF_KR + SSM_WIDTH
IN_WIDTH = OFF_U + N_BRANCH * D_MODEL

kernel_name = "hybrid_mla_s5_gated_block"


def rms_norm(x, g):
    xf = x.astype(jnp.float32)
    y = xf * lax.rsqrt(jnp.mean(xf * xf, axis=-1, keepdims=True) + EPS)
    return y.astype(x.dtype) * g.astype(x.dtype)


def rope_tables(length):
    half = QK_ROPE // 2
    freqs = ROPE_THETA ** (-jnp.arange(half, dtype=jnp.float32) / half)
    ang = jnp.arange(length, dtype=jnp.float32)[:, None] * freqs[None, :]
    return jnp.cos(ang), jnp.sin(ang)


def apply_rope(x, cos, sin):
    half = QK_ROPE // 2
    x1, x2 = x[..., :half], x[..., half:]
    c, s = cos.astype(x.dtype), sin.astype(x.dtype)
    return jnp.concatenate([x1 * c - x2 * s, x2 * c + x1 * s], axis=-1)


def mla_attention(q_nope, q_rope, k_nope, k_rope, v):
    length = q_nope.shape[1]
    scale = (QK_NOPE + QK_ROPE) ** -0.5
    outs = []
    for start in range(0, length, Q_BLOCK):
        end = min(start + Q_BLOCK, length)
        s = (jnp.einsum('bqhd,bkhd->bhqk', q_nope[:, start:end], k_nope[:, :end])
             + jnp.einsum('bqhr,bkr->bhqk', q_rope[:, start:end], k_rope[:, :end]))
        s = s.astype(jnp.float32) * scale
        q_pos = jnp.arange(start, end)[:, None]
        k_pos = jnp.arange(end)[None, :]
        s = jnp.where(k_pos <= q_pos, s, -jnp.inf)
        p = jax.nn.softmax(s, axis=-1).astype(v.dtype)
        outs.append(jnp.einsum('bhqk,bkhd->bqhd', p, v[:, :end]))
    return jnp.concatenate(outs, axis=1)


def s5_scan(u, lam_re, lam_im, log_step, b_re, b_im, c_re, c_im, d):
    f32 = jnp.float32
    step = jnp.exp(log_step.astype(f32))[:, None]
    lr, li = lam_re.astype(f32), lam_im.astype(f32)
    mag = jnp.exp(lr * step)
    ab_re, ab_im = mag * jnp.cos(li * step), mag * jnp.sin(li * step)
    den = lr * lr + li * li
    nr, ni = ab_re - 1.0, ab_im
    coef_re = (nr * lr + ni * li) / den
    coef_im = (ni * lr - nr * li) / den
    br, bi = b_re.astype(f32), b_im.astype(f32)
    bb_re = coef_re[..., None] * br - coef_im[..., None] * bi
    bb_im = coef_re[..., None] * bi + coef_im[..., None] * br
    uf = u.astype(f32)
    bu_re = jnp.einsum('blgp,gnp->blgn', uf, bb_re)
    bu_im = jnp.einsum('blgp,gnp->blgn', uf, bb_im)
    a_re = jnp.broadcast_to(ab_re, bu_re.shape)
    a_im = jnp.broadcast_to(ab_im, bu_im.shape)

    def combine(e1, e2):
        a1r, a1i, b1r, b1i = e1
        a2r, a2i, b2r, b2i = e2
        return (a2r * a1r - a2i * a1i,
                a2r * a1i + a2i * a1r,
                a2r * b1r - a2i * b1i + b2r,
                a2r * b1i + a2i * b1r + b2i)

    _, _, x_re, x_im = lax.associative_scan(combine, (a_re, a_im, bu_re, bu_im), axis=1)
    y = (jnp.einsum('blgn,gpn->blgp', x_re, c_re.astype(f32))
         - jnp.einsum('blgn,gpn->blgp', x_im, c_im.astype(f32))
         + d.astype(f32) * uf)
    return y


def setup_inputs(seed: int = 0) -> dict:
    key = jax.random.key(seed)
    ks = jax.random.split(key, 32)
    f32 = jnp.float32
    nrm = lambda k, shape, scale: jax.random.normal(k, shape, f32) * scale
    gain = lambda k, shape: 1.0 + 0.02 * jax.random.normal(k, shape, f32)
    G, N, P = SSM_GROUPS, SSM_STATE, SSM_GROUP
    lam_re = -0.5 * jnp.exp(0.05 * jax.random.normal(ks[9], (DEPTH, G, N), f32))
    lam_im = jnp.broadcast_to(math.pi * jnp.arange(N, dtype=f32), (DEPTH, G, N))
    log_step = jax.random.uniform(ks[10], (DEPTH, G), f32, math.log(DT_MIN), math.log(DT_MAX))
    return {
        "x": nrm(ks[0], (BATCH, SEQ, D_MODEL), 1.0),
        "meta_tokens": nrm(ks[1], (N_META, D_MODEL), 1.0),
        "norm_mix": gain(ks[2], (DEPTH, D_MODEL)),
        "w_in": nrm(ks[3], (DEPTH, D_MODEL, IN_WIDTH), D_MODEL ** -0.5),
        "norm_q": gain(ks[4], (DEPTH, Q_LORA)),
        "w_q_up": nrm(ks[5], (DEPTH, Q_LORA, MLA_HEADS * (QK_NOPE + QK_ROPE)), Q_LORA ** -0.5),
        "norm_kv": gain(ks[6], (DEPTH, KV_LORA)),
        "w_kv_up": nrm(ks[7], (DEPTH, KV_LORA, MLA_HEADS * (QK_NOPE + V_HEAD)), KV_LORA ** -0.5),
        "w_attn_proj": nrm(ks[8], (DEPTH, MLA_HEADS * V_HEAD, D_MODEL), (MLA_HEADS * V_HEAD) ** -0.5),
        "ssm_lambda_re": lam_re,
        "ssm_lambda_im": lam_im,
        "ssm_log_step": log_step,
        "ssm_b_re": nrm(ks[11], (DEPTH, G, N, P), (2 * P) ** -0.5),
        "ssm_b_im": nrm(ks[12], (DEPTH, G, N, P), (2 * P) ** -0.5),
        "ssm_c_re": nrm(ks[13], (DEPTH, G, P, N), N ** -0.5),
        "ssm_c_im": nrm(ks[14], (DEPTH, G, P, N), N ** -0.5),
        "ssm_d": nrm(ks[15], (DEPTH, G, P), 1.0),
        "w_glu_val": nrm(ks[16], (DEPTH, SSM_WIDTH, D_MODEL), SSM_WIDTH ** -0.5),
        "w_glu_gate": nrm(ks[17], (DEPTH, SSM_WIDTH, D_MODEL), SSM_WIDTH ** -0.5),
        "w_out": nrm(ks[18], (DEPTH, D_MODEL, D_MODEL), D_MODEL ** -0.5),
        "norm_ffn": gain(ks[19], (DEPTH, D_MODEL)),
        "w_ffn_gate": nrm(ks[20], (DEPTH, D_MODEL, D_FF), D_MODEL ** -0.5),
        "w_ffn_up": nrm(ks[21], (DEPTH, D_MODEL, D_FF), D_MODEL ** -0.5),
        "w_ffn_down": nrm(ks[22], (DEPTH, D_FF, D_MODEL), D_FF ** -0.5),
        "norm_final": gain(ks[23], (D_MODEL,)),
    }


def reference(x, meta_tokens, norm_mix, w_in, norm_q, w_q_up, norm_kv, w_kv_up, w_attn_proj,
              ssm_lambda_re, ssm_lambda_im, ssm_log_step, ssm_b_re, ssm_b_im, ssm_c_re, ssm_c_im,
              ssm_d, w_glu_val, w_glu_gate, w_out, norm_ffn, w_ffn_gate, w_ffn_up, w_ffn_down,
              norm_final):
    bsz = x.shape[0]
    meta = jnp.broadcast_to(meta_tokens[None].astype(x.dtype), (bsz, N_META, D_MODEL))
    h = jnp.concatenate([meta, x], axis=1)
    length = h.shape[1]
    cos, sin = rope_tables(length)
    for l in range(DEPTH):
        n = rms_norm(h, norm_mix[l])
        z = n @ w_in[l]
        q_lat, kv_lat, k_r, u, gates = jnp.split(z, [OFF_Q, OFF_KV, OFF_KR, OFF_U], axis=-1)
        q = (rms_norm(q_lat, norm_q[l]) @ w_q_up[l]).reshape(bsz, length, MLA_HEADS, QK_NOPE + QK_ROPE)
        kv = (rms_norm(kv_lat, norm_kv[l]) @ w_kv_up[l]).reshape(bsz, length, MLA_HEADS, QK_NOPE + V_HEAD)
        q_nope = q[..., :QK_NOPE]
        q_rope = apply_rope(q[..., QK_NOPE:], cos[:, None, :], sin[:, None, :])
        k_nope, v = kv[..., :QK_NOPE], kv[..., QK_NOPE:]
        k_rope = apply_rope(k_r, cos, sin)
        attn = mla_attention(q_nope, q_rope, k_nope, k_rope, v).reshape(bsz, length, MLA_HEADS * V_HEAD)
        attn_out = attn @ w_attn_proj[l]
        y = s5_scan(u.reshape(bsz, length, SSM_GROUPS, SSM_GROUP),
                    ssm_lambda_re[l], ssm_lambda_im[l], ssm_log_step[l],
                    ssm_b_re[l], ssm_b_im[l], ssm_c_re[l], ssm_c_im[l], ssm_d[l])
        y = jax.nn.gelu(y.reshape(bsz, length, SSM_WIDTH).astype(h.dtype))
        ssm_out = (y @ w_glu_val[l]) * jax.nn.sigmoid(y @ w_glu_gate[l])
        g = gates.reshape(bsz, length, N_BRANCH, D_MODEL)
        mixed = jax.nn.sigmoid(g[:, :, 0]) * attn_out + jax.nn.sigmoid(g[:, :, 1]) * ssm_out
        h = h + mixed @ w_out[l]
        m = rms_norm(h, norm_ffn[l])
        h = h + (jax.nn.silu(m @ w_ffn_gate[l]) * (m @ w_ffn_up[l])) @ w_ffn_down[l]
    out = rms_norm(h, norm_final)
    return out[:, N_META:]
```

```python
import contextlib
import math
import numpy as np
import ml_dtypes
import concourse.bass as bass
import concourse.mybir as mybir
from concourse.bass_utils import run_bass_kernel_spmd

F32 = mybir.dt.float32
BF16 = mybir.dt.bfloat16
I32 = mybir.dt.int32
AF = mybir.ActivationFunctionType
ALU = mybir.AluOpType
AX = mybir.AxisListType

D = 2048
KT = 16
NMETA = 16
SEQ = 4096
LALL = NMETA + SEQ
NOWN = 1024
EPS = 1e-6
EPOCH = 6000
NCORES = 8


class K:
    def __init__(self, nc, stack):
        self.nc = nc
        self.stack = stack
        self.names = ["pe", "act", "dve", "pool", "sp"]
        self.q = {n: [] for n in self.names}
        self.cnt = {n: 0 for n in self.names}
        self.epoch_sem = {n: None for n in self.names}
        self.waited = {n: {} for n in self.names}
        self.last_w = {}
        self.readers = {}
        self.dma_sem = {}
        self.dma_cnt = {}
        self.nsem = 0
        self.final_tokens = []
        self.pending = {n: [] for n in self.names}
        self.last_tok = {}

    def barrier(self):
        toks = list(self.last_tok.values()) + [(sem, self.dma_cnt[key]) for key, sem in self.dma_sem.items()]
        for n in self.names:
            self.pending[n] = list(toks)

    def new_sem(self, name):
        self.nsem += 1
        return self.stack.enter_context(self.nc.semaphore(f"s{self.nsem}_{name}"))

    def _eng_token(self, eng):
        if self.cnt[eng] % EPOCH == 0:
            self.epoch_sem[eng] = self.new_sem(eng)
        self.cnt[eng] += 1
        return (self.epoch_sem[eng], (self.cnt[eng] - 1) % EPOCH + 1)

    def _deps(self, eng, reads, writes, same_ok):
        deps = {}

        def add(tok, src):
            if src == eng and same_ok:
                return
            sem, val = tok
            key = id(sem)
            if key not in deps or deps[key][1] < val:
                deps[key] = (sem, val)

        for r in reads:
            if r in self.last_w:
                add(*self.last_w[r])
        for w in writes:
            if w in self.last_w:
                add(*self.last_w[w])
            for tok, src in self.readers.get(w, []):
                add(tok, src)
        for sem, val in self.pending[eng]:
            if id(sem) not in deps or deps[id(sem)][1] < val:
                deps[id(sem)] = (sem, val)
        self.pending[eng] = []
        out = []
        wd = self.waited[eng]
        for key, (sem, val) in deps.items():
            if wd.get(key, 0) >= val:
                continue
            wd[key] = val
            out.append((sem, val))
        return out

    def _commit(self, tok, src, reads, writes):
        for r in reads:
            self.readers.setdefault(r, []).append((tok, src))
        for w in writes:
            self.last_w[w] = (tok, src)
            self.readers[w] = []

    def op(self, eng, method, reads=(), writes=(), **kw):
        waits = self._deps(eng, reads, writes, eng == "pe")
        tok = self._eng_token(eng)
        self.last_tok[eng] = tok
        self.q[eng].append((waits, (method, kw), tok[0], 1))
        self._commit(tok, eng, reads, writes)

    def dma(self, eng, out, in_, reads=(), writes=(), key=None, **kw):
        if key is None:
            key = writes[0]
        if key not in self.dma_sem:
            self.dma_sem[key] = self.new_sem("dma")
            self.dma_cnt[key] = 0
        waits = self._deps(eng, reads, writes, False)
        self.dma_cnt[key] += 16
        sem = self.dma_sem[key]
        tok = (sem, self.dma_cnt[key])
        self.q[eng].append((waits, ("dma_start", dict(out=out, in_=in_, **kw)), sem, 16))
        self._commit(tok, "dma", reads, writes)
        return tok

    def emit(self, block):
        fin = list(self.final_tokens)
        qs = self.q

        def run(e, lst, final=None):
            for waits, fn, sem, amt in lst:
                for s, v in waits:
                    e.wait_ge(s, v)
                getattr(e, fn[0])(**fn[1]).then_inc(sem, amt)
            for s, v in (final or []):
                e.wait_ge(s, v)

        @block.sync
        def _(e):
            run(e, qs["sp"], fin)

        @block.tensor
        def _(e):
            run(e, qs["pe"])

        @block.scalar
        def _(e):
            run(e, qs["act"])

        @block.vector
        def _(e):
            run(e, qs["dve"])

        @block.gpsimd
        def _(e):
            run(e, qs["pool"])


class Arena:
    SZ = {F32: 4, I32: 4, BF16: 2}

    def __init__(self, ap, nwords):
        self.ap, self.n, self.off, self.peak = ap, nwords, 0, 0

    def alloc(self, name, shape, dt=F32):
        free = int(np.prod(shape[1:]))
        words = (free * self.SZ[dt] + 3) // 4
        assert self.off + words <= self.n, f"arena overflow allocating {name} {shape}: off={self.off} words={words} cap={self.n}"
        v = self.ap[0:shape[0], self.off:self.off + words]
        self.off += words
        self.peak = max(self.peak, self.off)
        if dt != F32:
            v = v.bitcast(dt)[:, 0:free]
        if len(shape) == 3:
            v = v.rearrange("p (a b) -> p a b", a=shape[1])
        elif len(shape) == 4:
            v = v.rearrange("p (a b c) -> p a b c", a=shape[1], b=shape[2])
        return v

    def mark(self):
        return self.off

    def release(self, m):
        self.off = m


def host_layout(inp):
    f32 = np.float32
    x = np.asarray(inp["x"], f32)
    meta = np.asarray(inp["meta_tokens"], f32)
    w_in = np.asarray(inp["w_in"], f32)[0]
    shared = {}
    kr = w_in[:, 1024:1088]
    krsw = np.concatenate([kr[:, 32:], kr[:, :32]], axis=1)
    wall = np.concatenate([w_in[:, 512:1024], kr, krsw, w_in[:, 1088:2112]], axis=1)
    shared["wall"] = np.ascontiguousarray(wall.reshape(KT, 128, 13, 128).transpose(2, 1, 0, 3))
    wown = np.concatenate([w_in[:, 0:512], w_in[:, 1088:2112], w_in[:, 2112:6208]], axis=1)
    shared["wown"] = np.ascontiguousarray(wown.reshape(KT, 128, 44, 128).transpose(2, 1, 0, 3))
    wq = np.asarray(inp["w_q_up"], f32)[0].reshape(512, 16, 192)
    wq = np.concatenate([wq[:, :, 0:128], wq[:, :, 128:192], wq[:, :, 160:192], wq[:, :, 128:160]], axis=2)
    shared["wq"] = np.ascontiguousarray(wq.reshape(4, 128, 16, 256).transpose(2, 1, 0, 3))
    wkv = np.asarray(inp["w_kv_up"], f32)[0].reshape(512, 16, 256)
    shared["wkv"] = np.ascontiguousarray(wkv.reshape(4, 128, 16, 256).transpose(2, 1, 0, 3))
    mt_layout = lambda w, nk: np.ascontiguousarray(w.reshape(nk, 128, w.shape[1] // 128, 128).transpose(2, 1, 0, 3))
    shared["wap"] = mt_layout(np.asarray(inp["w_attn_proj"], f32)[0], 16)
    shared["wgv"] = mt_layout(np.asarray(inp["w_glu_val"], f32)[0], 8)
    shared["wgg"] = mt_layout(np.asarray(inp["w_glu_gate"], f32)[0], 8)
    shared["wout"] = np.ascontiguousarray(np.asarray(inp["w_out"], f32)[0].reshape(16, 128, 4, 512).transpose(2, 1, 0, 3))
    shared["wfg"] = mt_layout(np.asarray(inp["w_ffn_gate"], f32)[0], 16)
    shared["wfu"] = mt_layout(np.asarray(inp["w_ffn_up"], f32)[0], 16)
    shared["wfd"] = np.ascontiguousarray(np.asarray(inp["w_ffn_down"], f32)[0].reshape(4, 11, 128, 4, 512).transpose(0, 3, 2, 1, 4))
    shared["gffnT"] = np.ascontiguousarray(np.asarray(inp["norm_ffn"], f32)[0].reshape(KT, 128).T)
    shared["gfin"] = np.ascontiguousarray(np.broadcast_to(np.asarray(inp["norm_final"], f32)[None, :], (128, D)))
    shared["gqT"] = np.ascontiguousarray(np.asarray(inp["norm_q"], f32)[0].reshape(4, 128).T)
    shared["gmixT"] = np.ascontiguousarray(np.asarray(inp["norm_mix"], f32)[0].reshape(KT, 128).T)
    shared["gkvT"] = np.ascontiguousarray(np.asarray(inp["norm_kv"], f32)[0].reshape(4, 128).T)
    shared["ident"] = np.eye(128, dtype=f32)
    half = 32
    freqs = (10000.0 ** (-np.arange(half, dtype=f32) / f32(half))).astype(f32)
    ang = (np.arange(LALL, dtype=f32)[:, None] * freqs[None, :]).astype(f32)
    cos, sin = np.cos(ang).astype(f32).T, np.sin(ang).astype(f32).T
    shared["cosA"] = np.ascontiguousarray(np.concatenate([cos, cos], 0))
    shared["sinA"] = np.ascontiguousarray(np.concatenate([-sin, sin], 0))
    lr = np.asarray(inp["ssm_lambda_re"], f32)[0]; li = np.asarray(inp["ssm_lambda_im"], f32)[0]
    dup = lambda a: np.ascontiguousarray(np.concatenate([a.T, a.T], 0))
    shared["lamr"] = dup(lr); shared["lami"] = dup(li)
    shared["lstep"] = np.ascontiguousarray(np.broadcast_to(np.asarray(inp["ssm_log_step"], f32)[0][None, :], (128, 64)))
    bre = np.asarray(inp["ssm_b_re"], f32)[0].transpose(1, 0, 2); bim = np.asarray(inp["ssm_b_im"], f32)[0].transpose(1, 0, 2)
    shared["Bst"] = np.ascontiguousarray(np.concatenate([bre, bim], 0).reshape(128, 1024))
    shared["Bsw"] = np.ascontiguousarray(np.concatenate([bim, bre], 0).reshape(128, 1024))
    cre = np.asarray(inp["ssm_c_re"], f32)[0].transpose(2, 0, 1); cim = np.asarray(inp["ssm_c_im"], f32)[0].transpose(2, 0, 1)
    shared["Ccat"] = np.ascontiguousarray(np.concatenate([cre, cim], 0).reshape(128, 1024))
    shared["Csw"] = np.ascontiguousarray(np.concatenate([cim, cre], 0).reshape(128, 1024))
    sgn = np.concatenate([-np.ones(64, f32), np.ones(64, f32)])
    pidx = np.arange(128)
    cst = np.zeros((128, 8), f32)
    cst[:, 0] = sgn; cst[:, 1] = -sgn; cst[:, 2] = ((pidx // 16) % 2 == 0); cst[:, 3] = ((pidx // 16) % 2 == 1)
    shared["cst"] = cst
    shared["dcol"] = np.ascontiguousarray(np.asarray(inp["ssm_d"], f32)[0].reshape(8, 128).T)
    shared["maskbd"] = np.kron(np.eye(8, dtype=f32), np.ones((16, 16), f32))
    J = np.zeros((128, 128), f32); J[(np.arange(128) + 64) % 128, np.arange(128)] = 1.0
    shared["Jmat"] = J
    maps = []
    for core in range(NCORES):
        b, c = divmod(core, 4)
        m = dict(shared)
        m["xa"] = np.ascontiguousarray(np.concatenate([meta, x[b]], 0))
        m["xo"] = np.ascontiguousarray(x[b].reshape(8, 4, 128, D)[:, c].reshape(NOWN, D))
        pos = (NMETA + 128 * (4 * np.arange(8)[:, None] + c) + np.arange(128)[None, :]).reshape(-1)
        m["cosO"] = np.ascontiguousarray(shared["cosA"][:, pos]); m["sinO"] = np.ascontiguousarray(shared["sinA"][:, pos])
        kk_, qq_ = np.arange(128)[:, None], np.arange(128)[None, :]
        msk = np.zeros((128, 4, 128), f32)
        for mm in range(4):
            msk[:, mm, :] = 1.0 if mm < c else ((kk_ <= qq_).astype(f32) if mm == c else 0.0)
        m["mskT"] = msk.reshape(128, 512)
        sel = np.zeros((128, 4), f32); sel[:, c] = 1.0
        m["sel"] = sel
        maps.append(m)
    return maps


def build(stage=99):
    nc = bass.Bass("TRN2", target_bir_lowering=False)
    dr = lambda name, shape, dt=F32, kind="ExternalInput": nc.dram_tensor(name, shape, dt, kind=kind).ap()
    xa = dr("xa", [LALL, D])
    wall = dr("wall", [13, 128, KT, 128])
    wall_bf = dr("wall_bf", [13, 128, KT, 128], BF16, "Internal")
    gmixT_d = dr("gmixT", [128, KT])
    gkvT_d = dr("gkvT", [128, 4])
    ident_d = dr("ident", [128, 128])
    cosA = dr("cosA", [64, LALL])
    sinA = dr("sinA", [64, LALL])
    HALF_COLS = [NMETA + 2048, 2048]
    lamr_d = dr("lamr", [128, 64]); lami_d = dr("lami", [128, 64]); lstep_d = dr("lstep", [128, 64])
    Bst_d = dr("Bst", [128, 1024]); Bsw_d = dr("Bsw", [128, 1024]); Ccat_d = dr("Ccat", [128, 1024]); Csw_d = dr("Csw", [128, 1024])
    cst_d = dr("cst", [128, 8]); dcol_d = dr("dcol", [128, 8]); maskbd_d = dr("maskbd", [128, 128]); Jmat_d = dr("Jmat", [128, 128])
    sel_d = dr("sel", [128, 4])
    wap_d = dr("wap", [16, 128, 16, 128]); wgv_d = dr("wgv", [16, 128, 8, 128]); wgg_d = dr("wgg", [16, 128, 8, 128])
    wout_d = dr("wout", [4, 128, 16, 512]); wfg_d = dr("wfg", [44, 128, 16, 128]); wfu_d = dr("wfu", [44, 128, 16, 128])
    wfd_d = dr("wfd", [4, 4, 128, 11, 512]); gffnT_d = dr("gffnT", [128, KT]); gfin_d = dr("gfin", [128, D])
    out_d = dr("out", [NOWN, D], F32, "ExternalOutput") if stage >= 6 else None
    wq_d = dr("wq", [16, 128, 4, 256]); wkv_d = dr("wkv", [16, 128, 4, 256])
    cosO_d = dr("cosO", [64, NOWN]); sinO_d = dr("sinO", [64, NOWN]); mskT_d = dr("mskT", [128, 512])
    xo = dr("xo", [NOWN, D]); wown = dr("wown", [44, 128, KT, 128]); gqT_d = dr("gqT", [128, 4])
    dbg = {}
    if stage == 2:
        dbg["stiles"] = dr("dbg_stiles", [128, 64 * 32], F32, "ExternalOutput")
        dbg["sown"] = dr("dbg_sown", [128, 64 * 8], F32, "ExternalOutput")
        dbg["pw"] = dr("dbg_pw", [128, 4 * 64], F32, "ExternalOutput")
    if stage == 5:
        dbg["mixed"] = dr("dbg_mixed", [128, 16 * NOWN], BF16, "ExternalOutput")
    if stage == 4:
        dbg["attn"] = dr("dbg_attn", [128, 16 * NOWN], BF16, "ExternalOutput")
    if stage == 3:
        dbg["y"] = dr("dbg_y", [128, 8 * NOWN], F32, "ExternalOutput")
        dbg["qn"] = dr("dbg_qn", [128, 4 * NOWN], BF16, "ExternalOutput")
    if stage == 1:
        dbg["kvn"] = dr("dbg_kvn", [128, 4 * LALL], BF16, "ExternalOutput")
        dbg["krope"] = dr("dbg_krope", [64, LALL], BF16, "ExternalOutput")
        dbg["u0"] = dr("dbg_u0", [128, 8 * HALF_COLS[0]], BF16, "ExternalOutput")
        dbg["u1"] = dr("dbg_u1", [128, 8 * HALF_COLS[1]], BF16, "ExternalOutput")

    with contextlib.ExitStack() as st:
        k = K(nc, st)
        AW = 52900
        arena_t = st.enter_context(nc.sbuf_tensor("arena", [128, AW], F32))
        arena = Arena(arena_t[:, :], AW)
        sb = arena.alloc
        pT = st.enter_context(nc.psum_tensor("pT", [128, KT * 128], BF16))
        pb = [st.enter_context(nc.psum_tensor(f"pb{i}", [128, 512], F32)) for i in range(6)]

        ident_f = sb("ident_f", [128, 128])
        ident_b = sb("ident_b", [128, 128], BF16)
        ones_b = sb("ones_b", [128, 128], BF16)
        gmixT = sb("gmixT_s", [128, KT])
        gkvT = sb("gkvT_s", [128, 4])
        gqT = sb("gqT_s", [128, 4])
        k.dma("sp", gqT[:], gqT_d, writes=["gqT"])
        k.dma("sp", ident_f[:], ident_d, writes=["ident_f"])
        k.dma("sp", gmixT[:], gmixT_d, writes=["gmixT"])
        k.dma("sp", gkvT[:], gkvT_d, writes=["gkvT"])
        k.op("dve", "tensor_copy", reads=["ident_f"], writes=["ident_b"], out=ident_b[:], in_=ident_f[:])
        k.op("dve", "memset", writes=["ones_b"], ap=ones_b[:], constant=1.0)
        r1_lo = arena.mark()
        kvnT = sb("kvnT", [128, 4, LALL], BF16)
        kropeT = sb("kropeT", [64, LALL], BF16)
        r1_hi = arena.mark()

        do_ssm = stage >= 2
        TWO_PI = 2.0 * math.pi
        R2W = 11264
        r2_ap = arena.alloc("R2", [128, R2W])
        r2 = Arena(r2_ap, R2W)
        ssb = r2.alloc
        cst = ssb("cst_s", [128, 8]); dcol = ssb("dcol_s", [128, 8]); maskbd = ssb("maskbd_s", [128, 128])
        Jmat = ssb("Jmat_s", [128, 128]); sel = ssb("sel_s", [128, 4])
        Cst = ssb("Cst_s", [128, 64, 16]); Csw2 = ssb("Csw2_s", [128, 64, 16]); Cst_b = ssb("Cst_b", [128, 1024], BF16)
        Xst = ssb("Xst", [128, 8, 1024], BF16)
        PRt = ssb("PRt", [128, 9, 64]); PIst = ssb("PIst", [128, 9, 64]); PIt = ssb("PIt", [128, 9, 64])
        S_own = ssb("S_own", [128, 64, 8]); Sw_own = ssb("Sw_own", [128, 64, 8])
        _small = {}

        def sm(name):
            if name not in _small:
                _small[name] = (ssb(name, [128, 64])[:, :], name)
            return _small[name]

        ta = sm("ta"); tb = sm("tb"); rr = sm("rr"); mm_ = sm("mm_")
        AR128 = sm("AR128"); AI128 = sm("AI128"); AIs128 = sm("AIs128"); AIs8 = sm("AIs8")
        M0 = arena.mark()
        WW = sb("WW", [128, 64, 2, 16])
        V128st = sb("V128st", [128, 64, 32]); VMst = sb("VMst", [128, 64, 2]); VMsw = sb("VMsw", [128, 64, 2])
        m_prep = arena.mark()
        QRt = sb("QRt", [128, 64, 16]); WIs = sb("WIs", [128, 64, 16])
        lamr = sb("lamr_s", [128, 64]); lami = sb("lami_s", [128, 64]); lstep = sb("lstep_s", [128, 64])
        Bst = sb("Bst_s", [128, 64, 16]); Bsw = sb("Bsw_s", [128, 64, 16])
        for t_, d_, tg in [(lamr, lamr_d, "lamr"), (lami, lami_d, "lami"), (lstep, lstep_d, "lstep"), (cst, cst_d, "cst"),
                           (dcol, dcol_d, "dcol"), (maskbd, maskbd_d, "maskbd"), (Jmat, Jmat_d, "Jmat"), (sel, sel_d, "sel")]:
            k.dma("sp", t_[:], d_, writes=[tg])
        for t_, d_, tg in [(Bst, Bst_d, "Bst"), (Bsw, Bsw_d, "Bsw"), (Cst, Ccat_d, "Cst"), (Csw2, Csw_d, "Csw2")]:
            k.dma("sp", t_[:].rearrange("p g q -> p (g q)"), d_, writes=[tg])
        sgn = cst[:, 0:1]; nsgn = cst[:, 1:2]
        parm = [cst[:, 2:3], cst[:, 3:4]]

        def tt(o, a, b, op):
            k.op("dve", "tensor_tensor", reads=[a[1], b[1]], writes=[o[1]], out=o[0], in0=a[0], in1=b[0], op=op)

        def tsc(o, a, s1, op0, s2=None, op1=None, extra=()):
            kw = dict(out=o[0], in0=a[0], scalar1=s1, scalar2=s2, op0=op0)
            if op1 is not None:
                kw["op1"] = op1
            k.op("dve", "tensor_scalar", reads=[a[1]] + list(extra), writes=[o[1]], **kw)

        def act(o, a, func, **kw):
            k.op("act", "activation", reads=[a[1]], writes=[o[1]], out=o[0], in_=a[0], func=func, **kw)

        if do_ssm:
            LR = (lamr[:, :], "lamr"); LI = (lami[:, :], "lami")
            pm = lambda nm: (sb(nm, [128, 64])[:, :], nm)
            step = pm("step"); lrs = pm("lrs"); lis = pm("lis"); mag = pm("mag"); sinv = pm("sinv"); cosv = pm("cosv")
            ar = pm("ar"); ai = pm("ai"); den = pm("den"); nr = pm("nr"); cr = pm("cr"); ci = pm("ci"); cis = pm("cis")
            ki = (sb("ki", [128, 64], I32)[:, :], "ki")
            act(step, (lstep[:, :], "lstep"), AF.Exp)
            tt(lrs, LR, step, ALU.mult)
            tt(lis, LI, step, ALU.mult)
            act(mag, lrs, AF.Exp)

            def sin_of(dst, src, shift):
                tsc(ta, src, shift, ALU.add)
                tsc(tb, ta, 1.0 / TWO_PI, ALU.mult)
                k.op("dve", "tensor_copy", reads=["tb"], writes=["ki"], out=ki[0], in_=tb[0])
                k.op("dve", "tensor_copy", reads=["ki"], writes=["tb"], out=tb[0], in_=ki[0])
                k.op("dve", "scalar_tensor_tensor", reads=["tb", "ta"], writes=["rr"], out=rr[0], in0=tb[0], scalar=-TWO_PI, in1=ta[0],
                     op0=ALU.mult, op1=ALU.add)
                tsc(mm_, rr, math.pi, ALU.is_gt)
                k.op("dve", "scalar_tensor_tensor", reads=["mm_", "rr"], writes=["ta"], out=ta[0], in0=mm_[0], scalar=-TWO_PI, in1=rr[0],
                     op0=ALU.mult, op1=ALU.add)
                tsc(mm_, ta, -math.pi, ALU.is_lt)
                k.op("dve", "scalar_tensor_tensor", reads=["mm_", "ta"], writes=["rr"], out=rr[0], in0=mm_[0], scalar=TWO_PI, in1=ta[0],
                     op0=ALU.mult, op1=ALU.add)
                tsc(rr, rr, -3.14159, ALU.max, 3.14159, ALU.min)
                act(dst, rr, AF.Sin)

            sin_of(sinv, lis, 0.0)
            sin_of(cosv, lis, math.pi / 2)
            tt(ar, mag, cosv, ALU.mult)
            tt(ai, mag, sinv, ALU.mult)
            tt(ta, LR, LR, ALU.mult); tt(den, LI, LI, ALU.mult); tt(den, den, ta, ALU.add)
            k.op("dve", "reciprocal", reads=["den"], writes=["den"], out=den[0], in_=den[0])
            tsc(nr, ar, -1.0, ALU.add)
            tt(ta, nr, LR, ALU.mult); tt(tb, ai, LI, ALU.mult); tt(ta, ta, tb, ALU.add); tt(cr, ta, den, ALU.mult)
            tt(ta, ai, LR, ALU.mult); tt(tb, nr, LI, ALU.mult); tt(ta, ta, tb, ALU.subtract); tt(ci, ta, den, ALU.mult)
            tsc(cis, ci, sgn, ALU.mult, extra=["cst"])

            bc16 = lambda t_: (t_[0].unsqueeze(2).to_broadcast([128, 64, 16]), t_[1])
            Bpst = (sb("Bpst", [128, 64, 16])[:], "Bpst"); Bpsw = (sb("Bpsw", [128, 64, 16])[:], "Bpsw")
            tx = (sb("tx", [128, 64, 16])[:], "tx"); ty = (sb("ty", [128, 64, 16])[:], "ty")
            BST = (Bst[:], "Bst"); BSW = (Bsw[:], "Bsw")
            tt(tx, BST, bc16(cr), ALU.mult); tt(ty, BSW, bc16(cis), ALU.mult); tt(Bpst, tx, ty, ALU.add)
            tt(tx, BSW, bc16(cr), ALU.mult); tt(ty, BST, bc16(cis), ALU.mult); tt(Bpsw, tx, ty, ALU.subtract)

            PR = lambda d_: (PRt[:, d_, :], "PRt"); PI = lambda d_: (PIt[:, d_, :], "PIt"); PIs = lambda d_: (PIst[:, d_, :], "PIst")
            k.op("dve", "memset", writes=["PRt"], ap=PRt[:, 0, :], constant=1.0)
            k.op("dve", "memset", writes=["PIt"], ap=PIt[:, 0, :], constant=0.0)
            k.op("dve", "tensor_copy", reads=["ar"], writes=["PRt"], out=PRt[:, 1, :], in_=ar[0])
            k.op("dve", "tensor_copy", reads=["ai"], writes=["PIt"], out=PIt[:, 1, :], in_=ai[0])

            def cmul(oR, oI, xR, xI, yR, yI):
                tt(ta, xR, yR, ALU.mult); tt(tb, xI, yI, ALU.mult); tt(oR, ta, tb, ALU.subtract)
                tt(ta, xR, yI, ALU.mult); tt(tb, xI, yR, ALU.mult); tt(oI, ta, tb, ALU.add)

            for d_ in range(2, 9):
                cmul(PR(d_), PI(d_), PR(d_ - 1), PI(d_ - 1), ar, ai)
            k.op("dve", "tensor_scalar", reads=["PIt", "cst"], writes=["PIst"], out=PIst[:].rearrange("p d g -> p (d g)"),
                 in0=PIt[:].rearrange("p d g -> p (d g)"), scalar1=sgn, scalar2=None, op0=ALU.mult)
            QIt = sb("QIt", [128, 64, 16])
            QR = lambda m_: (QRt[:, :, 15 - m_], "QRt"); QI = lambda m_: (QIt[:, :, 15 - m_], "QIt")
            k.op("dve", "memset", writes=["QRt"], ap=QRt[:, :, 15], constant=1.0)
            k.op("dve", "memset", writes=["QIt"], ap=QIt[:, :, 15], constant=0.0)
            for m_ in range(1, 16):
                cmul(QR(m_), QI(m_), QR(m_ - 1), QI(m_ - 1), PR(8), PI(8))
            cmul(AR128, AI128, QR(15), QI(15), PR(8), PI(8))
            tsc(AIs128, AI128, sgn, ALU.mult, extra=["cst"])
            tsc(AIs8, PI(8), sgn, ALU.mult, extra=["cst"])
            AR8 = PR(8)
            k.op("dve", "tensor_scalar", reads=["QIt", "cst"], writes=["WIs"], out=WIs[:].rearrange("p g q -> p (g q)"),
                 in0=QIt[:].rearrange("p g q -> p (g q)"), scalar1=sgn, scalar2=None, op0=ALU.mult)
            k.op("dve", "tensor_copy", reads=["QRt"], writes=["WW"], out=WW[:, :, 0, :], in_=QRt[:])
            k.op("dve", "tensor_copy", reads=["WIs"], writes=["WW"], out=WW[:, :, 1, :], in_=WIs[:])
            for d_ in range(8):
                tt(tx, Bpst, bc16(PR(d_)), ALU.mult); tt(ty, Bpsw, bc16(PIs(d_)), ALU.mult)
                tt((Xst[:, d_, :].rearrange("p (g q) -> p g q", q=16), "Xst"), tx, ty, ALU.add)
            k.op("dve", "tensor_scalar", reads=["Cst", "cst"], writes=["Cst"], out=Cst[:].rearrange("p g q -> p (g q)"),
                 in0=Cst[:].rearrange("p g q -> p (g q)"), scalar1=nsgn, scalar2=None, op0=ALU.mult)
            k.op("dve", "tensor_scalar", reads=["Csw2"], writes=["Csw2"], out=Csw2[:].rearrange("p g q -> p (g q)"),
                 in0=Csw2[:].rearrange("p g q -> p (g q)"), scalar1=-1.0, scalar2=None, op0=ALU.mult)
            k.op("dve", "tensor_copy", reads=["Cst"], writes=["Cst_b"], out=Cst_b[:], in_=Cst[:].rearrange("p g q -> p (g q)"))

            if stage == 2:
                pwt = sb("pwt", [128, 4, 64])
                for j_, src_ in enumerate([ar, ai, AR128, AI128]):
                    k.op("dve", "tensor_copy", reads=[src_[1]], writes=["pwt"], out=pwt[:, j_, :], in_=src_[0])
                k.final_tokens.append(k.dma("sp", dbg["pw"], pwt[:].rearrange("p j g -> p (j g)"), reads=["pwt"], writes=["dbg_pw"]))
        arena.release(m_prep)
        k.barrier()
        m_p1 = arena.mark()
        if do_ssm:
            Gp = [sb(f"Gp{i}", [128, 2, 8, 128], BF16) for i in range(1)]
            Gpsw = [sb(f"Gpsw{i}", [128, 2, 8, 128], BF16) for i in range(1)]
            vtb = [sb(f"vtb{i}", [128, 2, 16, 16]) for i in range(2)]
            gslot = [0]

            def gen_G(kt):
                s_ = 0
                for d_ in range(8):
                    k.op("pe", "transpose", reads=["Xst", "ident_b"], writes=["pT"],
                         out=pT[:, d_ * 128:(d_ + 1) * 128], in_=Xst[:, d_, kt * 128:(kt + 1) * 128], identity=ident_b[:, :])
                pv = pT[:, 0:1024].rearrange("p (d n) -> p d n", d=8)
                for par in range(2):
                    k.op("dve", "tensor_scalar", reads=["pT", "cst"], writes=[f"Gp{s_}"], out=Gp[s_][:, par, :, :], in0=pv,
                         scalar1=parm[par], scalar2=None, op0=ALU.mult)
                    k.op("dve", "tensor_scalar", reads=["pT", "cst"], writes=[f"Gpsw{s_}"], out=Gpsw[s_][:, par, :, 0:64], in0=pv[:, :, 64:128],
                         scalar1=parm[par], scalar2=None, op0=ALU.mult)
                    k.op("dve", "tensor_scalar", reads=["pT", "cst"], writes=[f"Gpsw{s_}"], out=Gpsw[s_][:, par, :, 64:128], in0=pv[:, :, 0:64],
                         scalar1=parm[par], scalar2=None, op0=ALU.mult)
                return s_

            def ssm_A(hf):
                xoff = NMETA if hf == 0 else 0
                for kt in range(8):
                    s_ = gen_G(kt)
                    for gl in range(8):
                        g = kt * 8 + gl
                        m_, par = gl // 2, gl % 2
                        rows = slice(32 * m_, 32 * m_ + 32)
                        B = 2 + gl % 4
                        uch = uTh[rows, kt, xoff:xoff + 2048].rearrange("p (c i) -> p c i", i=8)
                        for w_, (G_, gtag) in enumerate([(Gp[s_], f"Gp{s_}"), (Gpsw[s_], f"Gpsw{s_}")]):
                            for i in range(8):
                                k.op("pe", "matmul", reads=[gtag, "uTh"], writes=[f"pb{B}"],
                                     out=pb[B][:, w_ * 256:(w_ + 1) * 256], lhsT=G_[rows, par, 7 - i, :], rhs=uch[:, :, i],
                                     start=(i == 0), stop=(i == 7), tile_position=(32 * m_, 0))
                        if hf == 0:
                            Bm = gl % 2
                            um = uTh[rows, kt, 0:NMETA].rearrange("p (c i) -> p c i", i=8)
                            for w_, (G_, gtag) in enumerate([(Gp[s_], f"Gp{s_}"), (Gpsw[s_], f"Gpsw{s_}")]):
                                for i in range(8):
                                    k.op("pe", "matmul", reads=[gtag, "uTh"], writes=[f"pb{Bm}"],
                                         out=pb[Bm][:, w_ * 2:(w_ + 1) * 2], lhsT=G_[rows, par, 7 - i, :], rhs=um[:, :, i],
                                         start=(i == 0), stop=(i == 7), tile_position=(32 * m_, 0))
                            k.op("act", "activation", reads=[f"pb{Bm}"], writes=["VMst"], out=VMst[:, g, :], in_=pb[Bm][:, 0:2], func=AF.Identity)
                            k.op("act", "activation", reads=[f"pb{Bm}"], writes=["VMsw"], out=VMsw[:, g, :], in_=pb[Bm][:, 2:4], func=AF.Identity)
                        vs_ = g % 2
                        k.op("dve", "tensor_tensor", reads=[f"pb{B}", "WW"], writes=[f"vtb{vs_}"], out=vtb[vs_][:],
                             in0=pb[B][:, :].rearrange("p (w t q) -> p w t q", w=2, q=16),
                             in1=WW[:, g, :, :].unsqueeze(2).to_broadcast([128, 2, 16, 16]), op=ALU.mult)
                        if pend[0] is not None:
                            flush_pending()
                        pend[0] = (g, vs_, hf)
                if pend[0] is not None:
                    flush_pending()

            pend = [None]

            def flush_pending():
                g_, vs__, hf_ = pend[0]
                pend[0] = None
                k.op("dve", "tensor_reduce", reads=[f"vtb{vs__}"], writes=["V128st"], out=V128st[:, g_, hf_ * 16:hf_ * 16 + 16],
                     in_=vtb[vs__][:].rearrange("p w t q -> p t w q"), axis=AX.XY, op=ALU.add)

            def ssm_B():
                V128sw = sb("V128sw", [128, 64, 32])
                S_tiles = sb("S_tiles", [128, 64, 32]); Sw_tiles = sb("Sw_tiles", [128, 64, 32]); selt = sb("selt", [128, 512, 4])
                for q_ in range(4):
                    k.op("pe", "matmul", reads=["Jmat", "V128st"], writes=["pb2"], out=pb[2][:, :], lhsT=Jmat[:, :],
                         rhs=V128st[:].rearrange("p g t -> p (g t)")[:, q_ * 512:(q_ + 1) * 512], start=True, stop=True)
                    k.op("act", "activation", reads=["pb2"], writes=["V128sw"], out=V128sw[:].rearrange("p g t -> p (g t)")[:, q_ * 512:(q_ + 1) * 512],
                         in_=pb[2][:, :], func=AF.Identity)
                ST = lambda t_: (S_tiles[:, :, t_], "S_tiles"); SW = lambda t_: (Sw_tiles[:, :, t_], "Sw_tiles")
                vm = lambda T_, j_, tg: (T_[:, :, j_], tg)
                tt(ta, AR8, vm(VMst, 0, "VMst"), ALU.mult); tt(tb, AIs8, vm(VMsw, 0, "VMsw"), ALU.mult); tt(ta, ta, tb, ALU.add)
                tt(ST(0), ta, vm(VMst, 1, "VMst"), ALU.add)
                tt(ta, AR8, vm(VMsw, 0, "VMsw"), ALU.mult); tt(tb, AIs8, vm(VMst, 0, "VMst"), ALU.mult); tt(ta, ta, tb, ALU.subtract)
                tt(SW(0), ta, vm(VMsw, 1, "VMsw"), ALU.add)
                for t_ in range(31):
                    tt(ta, AR128, ST(t_), ALU.mult); tt(tb, AIs128, SW(t_), ALU.mult); tt(ta, ta, tb, ALU.add)
                    tt(rr, AR128, SW(t_), ALU.mult); tt(mm_, AIs128, ST(t_), ALU.mult); tt(rr, rr, mm_, ALU.subtract)
                    tt(ST(t_ + 1), ta, (V128st[:, :, t_], "V128st"), ALU.add)
                    tt(SW(t_ + 1), rr, (V128sw[:, :, t_], "V128sw"), ALU.add)
                for src_, stag, dst_, tg in [(S_tiles, "S_tiles", S_own, "S_own"), (Sw_tiles, "Sw_tiles", Sw_own, "Sw_own")]:
                    k.op("dve", "tensor_tensor", reads=[stag, "sel"], writes=["selt"], out=selt[:],
                         in0=src_[:].rearrange("p g (s c) -> p (g s) c", c=4), in1=sel[:, :].unsqueeze(1).to_broadcast([128, 512, 4]), op=ALU.mult)
                    k.op("dve", "tensor_reduce", reads=["selt"], writes=[tg], out=dst_[:].rearrange("p g s -> p (g s)"), in_=selt[:], axis=AX.X, op=ALU.add)
                if stage == 2:
                    k.final_tokens.append(k.dma("sp", dbg["stiles"], S_tiles[:].rearrange("p g t -> p (g t)"), reads=["S_tiles"], writes=["dbg_stiles"]))
                    k.final_tokens.append(k.dma("sp", dbg["sown"], S_own[:].rearrange("p g s -> p (g s)"), reads=["S_own"], writes=["dbg_sown"]))

        xt = [sb(f"xt{i}", [128, D]) for i in range(1)]
        xs = sb("xs", [128, D], BF16)
        junk = xs
        ss = sb("ss", [128, 1])
        rs = sb("rs", [128, 1])
        xT = sb("xT", [128, KT, 512], BF16)
        wch = [sb(f"wch{i}", [128, KT, 128], BF16) for i in range(2)]
        NW = len(wch)
        wslot = [0]

        def norm_transpose(src_rows, nt, slot, dst, dcol, gT, tag, sb_src=None, gtag="gmixT"):
            if sb_src is None:
                xb_ = xt[slot]
                xtag = f"xt{slot}"
                k.dma("sp", xb_[0:nt, :], src_rows, writes=[xtag])
            else:
                xb_, xtag = sb_src
            k.op("act", "activation", reads=[xtag], writes=["xs", "ss"],
                 out=junk[0:nt, :], in_=xb_[0:nt, :], func=AF.Square, accum_out=ss[0:nt, :])
            k.op("act", "activation", reads=["ss"], writes=["rs"],
                 out=rs[0:nt, :], in_=ss[0:nt, :], func=AF.Sqrt, scale=1.0 / D, bias=EPS)
            k.op("dve", "reciprocal", reads=["rs"], writes=["rs"], out=rs[0:nt, :], in_=rs[0:nt, :])
            k.op("dve", "tensor_scalar", reads=[xtag, "rs"], writes=["xs"],
                 out=xs[0:nt, :], in0=xb_[0:nt, :], scalar1=rs[0:nt, 0:1], scalar2=None, op0=ALU.mult)
            for kt in range(KT):
                k.op("pe", "transpose", reads=["xs", "ident_b"], writes=["pT"],
                     out=pT[:, kt * 128:kt * 128 + nt], in_=xs[0:nt, kt * 128:(kt + 1) * 128], identity=ident_b[0:nt, 0:nt])
            pv = pT[:].rearrange("p (k t) -> p k t", k=KT)
            k.op("dve", "tensor_tensor", reads=["pT", gtag], writes=[tag],
                 out=dst[:, :, dcol:dcol + nt], in0=pv[:, :, 0:nt],
                 in1=gT[:, :].unsqueeze(2).to_broadcast([128, KT, nt]), op=ALU.mult)

        def load_w(src_ap, bf_dst=None, bf_src=None, bf_tag=None):
            s_ = wslot[0] % NW
            wslot[0] += 1
            if bf_src is not None:
                k.dma("sp", wch[s_][:], bf_src, reads=[bf_tag], writes=[f"wch{s_}"])
            else:
                k.dma("pool", wch[s_][:], src_ap, writes=[f"wch{s_}"])
                if bf_dst is not None:
                    k.dma("sp", bf_dst, wch[s_][:], reads=[f"wch{s_}"], writes=[bf_tag])
            return wch[s_], f"wch{s_}"

        kvl = sb("kvl", [128, 4, 512])
        sq = sb("sq", [128, 4, 512], BF16)
        rkv = sb("rkv", [128, 512])
        cosb = sb("cosb", [64, 512])
        sinb = sb("sinb", [64, 512])
        t1 = sb("t1", [64, 512])
        t2 = sb("t2", [64, 512])
        uTh = sb("uTh", [128, 8, HALF_COLS[0]], BF16)

        blocks = [(0, NMETA)] + [(NMETA + 512 * i, 512) for i in range(8)]
        halves = [blocks[0:5], blocks[5:9]]
        bank = [0]

        def nxt_bank():
            b_ = 2 + bank[0] % 3
            bank[0] += 1
            return b_

        for hf, hblocks in enumerate(halves):
            hcol0 = hblocks[0][0]
            for (c0, n) in hblocks:
                ntile = max(1, n // 128)
                for ti in range(ntile):
                    nt = min(128, n)
                    norm_transpose(xa[c0 + ti * 128:c0 + ti * 128 + nt, :], nt, 0, xT, ti * 128, gmixT, "xT")
                k.dma("sp", cosb[:, 0:n], cosA[:, c0:c0 + n], writes=["cosb"])
                k.dma("sp", sinb[:, 0:n], sinA[:, c0:c0 + n], writes=["sinb"])
                for mt in range(13):
                    if c0 == 0:
                        wt, wtag = load_w(wall[mt], bf_dst=wall_bf[mt], bf_tag=f"wallbf{mt}")
                    else:
                        wt, wtag = load_w(None, bf_src=wall_bf[mt], bf_tag=f"wallbf{mt}")
                    if mt == 4:
                        for hh in range(2):
                            for kt in range(KT):
                                k.op("pe", "matmul", reads=[wtag, "xT"], writes=[f"pb{hh}"],
                                     out=pb[hh][0:64, 0:n], lhsT=wt[:, kt, hh * 64:hh * 64 + 64], rhs=xT[:, kt, 0:n],
                                     start=(kt == 0), stop=(kt == KT - 1))
                        k.op("dve", "tensor_tensor", reads=["pb0", "cosb"], writes=["t1"],
                             out=t1[:, 0:n], in0=pb[0][0:64, 0:n], in1=cosb[:, 0:n], op=ALU.mult)
                        k.op("dve", "tensor_tensor", reads=["pb1", "sinb"], writes=["t2"],
                             out=t2[:, 0:n], in0=pb[1][0:64, 0:n], in1=sinb[:, 0:n], op=ALU.mult)
                        k.op("dve", "tensor_tensor", reads=["t1", "t2"], writes=["kropeT"],
                             out=kropeT[:, c0:c0 + n], in0=t1[:, 0:n], in1=t2[:, 0:n], op=ALU.add)
                        continue
                    b = nxt_bank()
                    for kt in range(KT):
                        k.op("pe", "matmul", reads=[wtag, "xT"], writes=[f"pb{b}"],
                             out=pb[b][:, 0:n], lhsT=wt[:, kt, :], rhs=xT[:, kt, 0:n], start=(kt == 0), stop=(kt == KT - 1))
                    if mt < 4:
                        k.op("act", "activation", reads=[f"pb{b}"], writes=["kvl"],
                             out=kvl[:, mt, 0:n], in_=pb[b][:, 0:n], func=AF.Identity)
                        k.op("act", "activation", reads=[f"pb{b}"], writes=["sq"],
                             out=sq[:, mt, 0:n], in_=pb[b][:, 0:n], func=AF.Square)
                        if mt == 3:
                            for j in range(4):
                                k.op("pe", "matmul", reads=["ones_b", "sq"], writes=["pb5"],
                                     out=pb[5][:, 0:n], lhsT=ones_b[:, :], rhs=sq[:, j, 0:n], start=(j == 0), stop=(j == 3))
                            k.op("act", "activation", reads=["pb5"], writes=["rkv"],
                                 out=rkv[:, 0:n], in_=pb[5][:, 0:n], func=AF.Sqrt, scale=1.0 / 512, bias=EPS)
                            k.op("dve", "reciprocal", reads=["rkv"], writes=["rkv"], out=rkv[:, 0:n], in_=rkv[:, 0:n])
                            for j in range(4):
                                k.op("dve", "scalar_tensor_tensor", reads=["kvl", "rkv", "gkvT"], writes=["kvnT"],
                                     out=kvnT[:, j, c0:c0 + n], in0=kvl[:, j, 0:n], scalar=gkvT[:, j:j + 1], in1=rkv[:, 0:n],
                                     op0=ALU.mult, op1=ALU.mult)
                    else:
                        j = mt - 5
                        k.op("act", "activation", reads=[f"pb{b}"], writes=["uTh"],
                             out=uTh[:, j, c0 - hcol0:c0 - hcol0 + n], in_=pb[b][:, 0:n], func=AF.Identity)
            if do_ssm:
                ssm_A(hf)
            if stage == 1:
                nh = HALF_COLS[hf]
                k.final_tokens.append(k.dma("sp", dbg[f"u{hf}"].rearrange("p (j c) -> p j c", j=8), uTh[:, :, 0:nh], reads=["uTh"], writes=[f"dbg_u{hf}"]))

        if stage == 1:
            k.final_tokens.append(k.dma("sp", dbg["kvn"].rearrange("p (j c) -> p j c", j=4), kvnT[:], reads=["kvnT"], writes=["dbg_kvn"]))
            k.final_tokens.append(k.dma("sp", dbg["krope"], kropeT[:], reads=["kropeT"], writes=["dbg_krope"]))

        arena.release(m_p1)
        k.barrier()
        m_b = arena.mark()
        if do_ssm:
            ssm_B()
        arena.release(M0)
        k.barrier()
        qnT = sb("qnT", [128, 4, NOWN], BF16)
        xoT = sb("xoT", [128, KT, NOWN], BF16)
        gT = sb("gT", [128, 8, NOWN], BF16)
        m_own = arena.mark()
        if stage >= 3:
            u_ownT = sb("u_ownT", [128, 8, NOWN], BF16)
            m_o1 = arena.mark()
            xt = [sb("xt0", [128, D])]; xs = sb("xs", [128, D], BF16); junk = xs
            ss = sb("ss", [128, 1]); rs = sb("rs", [128, 1])
            wch = [sb(f"wch{i}", [128, KT, 128], BF16) for i in range(3)]; NW = 3
            ql = sb("ql", [128, 4, 512]); sq = sb("sq", [128, 4, 512], BF16); rq = sb("rq", [128, 512])
            for blk in range(2):
                for ti in range(4):
                    r0 = blk * 512 + ti * 128
                    norm_transpose(xo[r0:r0 + 128, :], 128, 0, xoT, r0, gmixT, "xoT")
                c0 = blk * 512
                for mt in range(12):
                    wt, wtag = load_w(wown[mt])
                    b = nxt_bank()
                    for kt in range(KT):
                        k.op("pe", "matmul", reads=[wtag, "xoT"], writes=[f"pb{b}"],
                             out=pb[b][:, :], lhsT=wt[:, kt, :], rhs=xoT[:, kt, c0:c0 + 512], start=(kt == 0), stop=(kt == KT - 1))
                    if mt < 4:
                        k.op("act", "activation", reads=[f"pb{b}"], writes=["ql"], out=ql[:, mt, :], in_=pb[b][:, :], func=AF.Identity)
                        k.op("act", "activation", reads=[f"pb{b}"], writes=["sq"], out=sq[:, mt, :], in_=pb[b][:, :], func=AF.Square)
                        if mt == 3:
                            for j in range(4):
                                k.op("pe", "matmul", reads=["ones_b", "sq"], writes=["pb5"],
                                     out=pb[5][:, :], lhsT=ones_b[:, :], rhs=sq[:, j, :], start=(j == 0), stop=(j == 3))
                            k.op("act", "activation", reads=["pb5"], writes=["rq"], out=rq[:, :], in_=pb[5][:, :], func=AF.Sqrt, scale=1.0 / 512, bias=EPS)
                            k.op("dve", "reciprocal", reads=["rq"], writes=["rq"], out=rq[:, :], in_=rq[:, :])
                            for j in range(4):
                                k.op("dve", "scalar_tensor_tensor", reads=["ql", "rq", "gqT"], writes=["qnT"],
                                     out=qnT[:, j, c0:c0 + 512], in0=ql[:, j, :], scalar=gqT[:, j:j + 1], in1=rq[:, :], op0=ALU.mult, op1=ALU.mult)
                    else:
                        k.op("act", "activation", reads=[f"pb{b}"], writes=["u_ownT"], out=u_ownT[:, mt - 4, c0:c0 + 512], in_=pb[b][:, :], func=AF.Identity)
            arena.release(m_o1)
            k.barrier()

            NB = 8
            Gp = [sb("GpC", [128, 2, 8, 128], BF16)]; Gpsw = [sb("GpswC", [128, 2, 8, 128], BF16)]
            V8o = [sb("V8o_st", [128, NB, 8, 16]), sb("V8o_sw", [128, NB, 8, 16])]
            S_all = sb("S_all", [128, NB, 8, 16], BF16)
            XX = [sb(f"XX{i}", [128, NB, 8]) for i in range(4)]
            c3 = [sb(f"c3_{i}", [128, NB, 8]) for i in range(4)]
            Hp = sb("Hp", [128, 8, 8, 32], BF16)
            h1 = sb("h1t", [128, 4, 8, 16]); h2 = sb("h2t", [128, 4, 8, 16])
            BDK = sb("BDK", [128, 8, 128], BF16); dmat = sb("dmat", [128, 128])
            ytmp = sb("ytmp", [128, NOWN]); g1 = sb("g1", [128, NOWN]); g2 = sb("g2", [128, NOWN])
            k.op("dve", "memset", writes=["Hp"], ap=Hp[:].rearrange("p a b c -> p (a b c)"), constant=0.0)
            bc8 = lambda ap_: ap_.unsqueeze(2).to_broadcast([128, NB, 8])
            for bt in range(8):
                g0 = bt * NB
                for kk_, kt in enumerate((bt,)):
                    gen_G(kt)
                    for gl in range(8):
                        gb = kk_ * 8 + gl
                        m_, par = gl // 2, gl % 2
                        rows = slice(32 * m_, 32 * m_ + 32)
                        B = 2 + gl % 4
                        uch = u_ownT[rows, kt, :].rearrange("p (c i) -> p c i", i=8)
                        for w_, (G_, gtag) in enumerate([(Gp[0], "Gp0"), (Gpsw[0], "Gpsw0")]):
                            for i in range(8):
                                k.op("pe", "matmul", reads=[gtag, "u_ownT"], writes=[f"pb{B}"],
                                     out=pb[B][:, w_ * 128:(w_ + 1) * 128], lhsT=G_[rows, par, 7 - i, :], rhs=uch[:, :, i],
                                     start=(i == 0), stop=(i == 7), tile_position=(32 * m_, 0))
                            k.op("act", "activation", reads=[f"pb{B}"], writes=[f"V8o{w_}"],
                                 out=V8o[w_][:, gb, :, :].rearrange("p s q -> p (s q)"), in_=pb[B][:, w_ * 128:(w_ + 1) * 128], func=AF.Identity)
                X, Xw, Xn, Xwn = XX
                k.op("dve", "tensor_copy", reads=["S_own"], writes=["XX0"], out=X[:], in_=S_own[:, g0:g0 + NB, :])
                k.op("dve", "tensor_copy", reads=["Sw_own"], writes=["XX1"], out=Xw[:], in_=Sw_own[:, g0:g0 + NB, :])
                tg = {id(XX[i]): f"XX{i}" for i in range(4)}
                A8 = bc8(PRt[:, 8, g0:g0 + NB]); A8s = bc8(AIs8[0][:, g0:g0 + NB])
                for kk in range(16):
                    k.op("act", "activation", reads=[tg[id(X)]], writes=["S_all"], out=S_all[:, :, :, kk], in_=X[:], func=AF.Identity)
                    if kk == 15:
                        break
                    dv = lambda o, a, b_, op, r, w: k.op("dve", "tensor_tensor", reads=r, writes=[w], out=o, in0=a, in1=b_, op=op)
                    dv(c3[0][:], X[:], A8, ALU.mult, [tg[id(X)], "PRt"], "c30")
                    dv(c3[1][:], Xw[:], A8s, ALU.mult, [tg[id(Xw)], "AIs8"], "c31")
                    dv(c3[0][:], c3[0][:], c3[1][:], ALU.add, ["c30", "c31"], "c30")
                    dv(Xn[:], c3[0][:], V8o[0][:, :, :, kk], ALU.add, ["c30", "V8o0"], tg[id(Xn)])
                    dv(c3[2][:], Xw[:], A8, ALU.mult, [tg[id(Xw)], "PRt"], "c32")
                    dv(c3[3][:], X[:], A8s, ALU.mult, [tg[id(X)], "AIs8"], "c33")
                    dv(c3[2][:], c3[2][:], c3[3][:], ALU.subtract, ["c32", "c33"], "c32")
                    dv(Xwn[:], c3[2][:], V8o[1][:, :, :, kk], ALU.add, ["c32", "V8o1"], tg[id(Xwn)])
                    X, Xn = Xn, X
                    Xw, Xwn = Xwn, Xw
                for kk_, kt in enumerate((bt,)):
                    for par in range(2):
                        pick = lambda T_: T_[:, kt * 8:(kt + 1) * 8, :].rearrange("p (m two) q -> p m two q", two=2)[:, :, par, :].unsqueeze(2).to_broadcast([128, 4, 8, 16])
                        pk2 = lambda T_: T_[:, 1:9, kt * 8:(kt + 1) * 8].rearrange("p j (m two) -> p m two j", two=2)[:, :, par, :].unsqueeze(3).to_broadcast([128, 4, 8, 16])
                        k.op("dve", "tensor_tensor", reads=["Cst", "PRt"], writes=["h1t"], out=h1[:], in0=pick(Cst), in1=pk2(PRt), op=ALU.mult)
                        k.op("dve", "tensor_tensor", reads=["Csw2", "PIt"], writes=["h2t"], out=h2[:], in0=pick(Csw2), in1=pk2(PIt), op=ALU.mult)
                        k.op("dve", "tensor_tensor", reads=["h1t", "h2t"], writes=["Hp"],
                             out=Hp[:].rearrange("p (m two) j q -> p m two j q", two=2)[:, :, par, :, par * 16:(par + 1) * 16],
                             in0=h1[:], in1=h2[:], op=ALU.add)
                    for d_ in range(8):
                        k.op("pe", "matmul", reads=["Xst", "Cst_b"], writes=[f"pb{d_ // 4}"],
                             out=pb[d_ // 4][:, (d_ % 4) * 128:(d_ % 4 + 1) * 128], lhsT=Xst[:, d_, kt * 128:(kt + 1) * 128],
                             rhs=Cst_b[:, kt * 128:(kt + 1) * 128], start=True, stop=True)
                    for hb in range(2):
                        k.op("dve", "tensor_tensor", reads=[f"pb{hb}", "maskbd"], writes=["BDK"], out=BDK[:, hb * 4:hb * 4 + 4, :],
                             in0=pb[hb][:, :].rearrange("p (d n) -> p d n", d=4), in1=maskbd[:, :].unsqueeze(1).to_broadcast([128, 4, 128]), op=ALU.mult)
                    k.op("dve", "tensor_scalar", reads=["ident_f", "dcol"], writes=["dmat"], out=dmat[:, :], in0=ident_f[:, :],
                         scalar1=dcol[:, kt:kt + 1], scalar2=None, op0=ALU.mult)
                    k.op("dve", "tensor_tensor", reads=["BDK", "dmat"], writes=["BDK"], out=BDK[:, 0, :], in0=BDK[:, 0, :], in1=dmat[:, :], op=ALU.add)
                    uch = u_ownT[:, kt, :].rearrange("p (c i) -> p c i", i=8)
                    for j in range(8):
                        yb = 4 + j // 4
                        reg = pb[yb][:, (j % 4) * 128:(j % 4 + 1) * 128]
                        for i in range(j + 1):
                            k.op("pe", "matmul", reads=["BDK", "u_ownT"], writes=[f"pb{yb}"], out=reg, lhsT=BDK[:, j - i, :], rhs=uch[:, :, i],
                                 start=(i == 0), stop=False)
                        for gl in range(8):
                            m_ = gl // 2
                            gb = kk_ * 8 + gl
                            k.op("pe", "matmul", reads=["Hp", "S_all"], writes=[f"pb{yb}"],
                                 out=pb[yb][32 * m_:32 * m_ + 32, (j % 4) * 128:(j % 4 + 1) * 128], lhsT=Hp[:, gl, j, :],
                                 rhs=S_all[:, gb, :, :].rearrange("p s q -> p (s q)"), start=False, stop=(gl % 2 == 1), tile_position=(0, 32 * m_))
                    for hb in range(2):
                        k.op("act", "activation", reads=[f"pb{4 + hb}"], writes=["ytmp"],
                             out=ytmp[:, :].rearrange("p (c j) -> p j c", j=8)[:, hb * 4:hb * 4 + 4, :],
                             in_=pb[4 + hb][:, :].rearrange("p (j c) -> p j c", j=4), func=AF.Identity)
                    if stage == 3:
                        k.final_tokens.append(k.dma("sp", dbg["y"][:, kt * NOWN:(kt + 1) * NOWN], ytmp[:, :], reads=["ytmp"], writes=[f"dbg_y{kt}"]))
                    k.op("act", "activation", reads=["ytmp"], writes=["g1"], out=g1[:, :], in_=ytmp[:, :], func=AF.Square)
                    k.op("dve", "tensor_scalar", reads=["g1"], writes=["g1"], out=g1[:, :], in0=g1[:, :], scalar1=0.044715, scalar2=1.0, op0=ALU.mult, op1=ALU.add)
                    k.op("dve", "tensor_tensor", reads=["g1", "ytmp"], writes=["g2"], out=g2[:, :], in0=g1[:, :], in1=ytmp[:, :], op=ALU.mult)
                    k.op("act", "activation", reads=["g2"], writes=["g1"], out=g1[:, :], in_=g2[:, :], func=AF.Sigmoid, scale=2.0 * math.sqrt(2.0 / math.pi))
                    k.op("dve", "tensor_tensor", reads=["g1", "ytmp"], writes=["gT"], out=gT[:, kt, :], in0=g1[:, :], in1=ytmp[:, :], op=ALU.mult)
            if stage == 3:
                k.final_tokens.append(k.dma("sp", dbg["qn"].rearrange("p (j c) -> p j c", j=4), qnT[:], reads=["qnT"], writes=["dbg_qn"]))
        arena.release(m_own)
        k.barrier()
        if stage >= 4:
            r2.release(0)
            attnT = r2.alloc("attnT", [128, 16, NOWN], BF16)
            m_att = arena.mark()
            KTb = sb("KTb", [128, LALL], BF16); Vsb = sb("Vsb", [128, 33, 128], BF16)
            qn_h = sb("qn_h", [128, NOWN], BF16); qro_h = sb("qro_h", [64, NOWN], BF16)
            cosO = sb("cosO_s", [64, NOWN]); sinO = sb("sinO_s", [64, NOWN]); mskT = sb("mskT_s", [128, 4, 128])
            a1 = sb("a1", [64, 512]); a2 = sb("a2", [64, 512]); rden = sb("rden", [128, 512])
            ptb = [sb(f"ptb{i}", [128, 512], BF16) for i in range(3)]
            wqb = [sb(f"wqb{i}", [128, 4, 256], BF16) for i in range(2)]; wkvb = [sb(f"wkvb{i}", [128, 4, 256], BF16) for i in range(2)]
            k.dma("sp", cosO[:, :], cosO_d, writes=["cosO"]); k.dma("sp", sinO[:, :], sinO_d, writes=["sinO"])
            k.dma("sp", mskT[:].rearrange("p m q -> p (m q)"), mskT_d, writes=["mskT"])
            SCALE = 1.0 / math.sqrt(192.0)
            ktiles = [(0, NMETA)] + [(NMETA + 128 * t_, 128) for t_ in range(32)]
            pti = [0]
            for h in range(16):
                hs = h % 2
                k.dma("pool", wqb[hs][:], wq_d[h], writes=[f"wqb{hs}"]); k.dma("pool", wkvb[hs][:], wkv_d[h], writes=[f"wkvb{hs}"])
                wq_, wkv_ = wqb[hs], wkvb[hs]
                for bi, (c0, n) in enumerate(blocks):
                    B = 4 + bi % 2
                    for kt in range(4):
                        k.op("pe", "matmul", reads=[f"wkvb{hs}", "kvnT"], writes=[f"pb{B}"], out=pb[B][:, 0:n], lhsT=wkv_[:, kt, 0:128],
                             rhs=kvnT[:, kt, c0:c0 + n], start=(kt == 0), stop=(kt == 3))
                    k.op("act", "activation", reads=[f"pb{B}"], writes=["KTb"], out=KTb[:, c0:c0 + n], in_=pb[B][:, 0:n], func=AF.Identity)
                for t0 in range(0, 33, 4):
                    B = 4 + (t0 // 4) % 2
                    tl = list(range(t0, min(t0 + 4, 33)))
                    for j_, t_ in enumerate(tl):
                        c0, nk = ktiles[t_]
                        for kt in range(4):
                            k.op("pe", "matmul", reads=[f"wkvb{hs}", "kvnT"], writes=[f"pb{B}"], out=pb[B][0:nk, j_ * 128:(j_ + 1) * 128],
                                 lhsT=kvnT[:, kt, c0:c0 + nk], rhs=wkv_[:, kt, 128:256], start=(kt == 0), stop=(kt == 3))
                    if t0 == 0:
                        k.op("act", "activation", reads=[f"pb{B}"], writes=["Vsb"], out=Vsb[0:NMETA, 0, :], in_=pb[B][0:NMETA, 0:128], func=AF.Identity)
                        k.op("act", "activation", reads=[f"pb{B}"], writes=["Vsb"], out=Vsb[:, 1:4, :],
                             in_=pb[B][:, 128:512].rearrange("p (t d) -> p t d", t=3), func=AF.Identity)
                    else:
                        nt_ = len(tl)
                        k.op("act", "activation", reads=[f"pb{B}"], writes=["Vsb"], out=Vsb[:, t0:t0 + nt_, :],
                             in_=pb[B][:, 0:nt_ * 128].rearrange("p (t d) -> p t d", t=nt_), func=AF.Identity)
                for qb in range(2):
                    qc = slice(qb * 512, (qb + 1) * 512)
                    for kt in range(4):
                        k.op("pe", "matmul", reads=[f"wqb{hs}", "qnT"], writes=["pb4"], out=pb[4][:, :], lhsT=wq_[:, kt, 0:128], rhs=qnT[:, kt, qc],
                             start=(kt == 0), stop=(kt == 3))
                    k.op("act", "activation", reads=["pb4"], writes=["qn_h"], out=qn_h[:, qc], in_=pb[4][:, :], func=AF.Identity)
                    for hh in range(2):
                        for kt in range(4):
                            k.op("pe", "matmul", reads=[f"wqb{hs}", "qnT"], writes=[f"pb{hh}"], out=pb[hh][0:64, :],
                                 lhsT=wq_[:, kt, 128 + 64 * hh:192 + 64 * hh], rhs=qnT[:, kt, qc], start=(kt == 0), stop=(kt == 3))
                    k.op("dve", "tensor_tensor", reads=["pb0", "cosO"], writes=["a1"], out=a1[:, :], in0=pb[0][0:64, :], in1=cosO[:, qc], op=ALU.mult)
                    k.op("dve", "tensor_tensor", reads=["pb1", "sinO"], writes=["a2"], out=a2[:, :], in0=pb[1][0:64, :], in1=sinO[:, qc], op=ALU.mult)
                    k.op("dve", "tensor_tensor", reads=["a1", "a2"], writes=["qro_h"], out=qro_h[:, qc], in0=a1[:, :], in1=a2[:, :], op=ALU.add)
                for qb in range(2):
                    s0 = 4 * qb
                    need = [0] + [1 + xt for xt in range(0, 4 * (s0 + 3) + 4)]
                    def stage_a(ni, t_):
                        c0, nk = ktiles[t_]
                        if t_ == 0:
                            cl, mslot = 0, None
                        else:
                            xt_ = t_ - 1
                            smin = max(s0, xt_ // 4)
                            cl = (smin - s0) * 128
                            mslot = xt_ // 4 if xt_ // 4 >= s0 else None
                        qcol = slice(qb * 512 + cl, (qb + 1) * 512)
                        SB = ni % 2
                        k.op("pe", "matmul", reads=["KTb", "qn_h"], writes=[f"pb{SB}"], out=pb[SB][0:nk, cl:512], lhsT=KTb[:, c0:c0 + nk], rhs=qn_h[:, qcol],
                             start=True, stop=False)
                        k.op("pe", "matmul", reads=["kropeT", "qro_h"], writes=[f"pb{SB}"], out=pb[SB][0:nk, cl:512], lhsT=kropeT[0:64, c0:c0 + nk],
                             rhs=qro_h[0:64, qcol], start=False, stop=True)
                        pt = ptb[pti[0] % 3]; ptag = f"ptb{pti[0] % 3}"; pti[0] += 1
                        k.op("act", "activation", reads=[f"pb{SB}"], writes=[ptag], out=pt[0:nk, cl:512], in_=pb[SB][0:nk, cl:512], func=AF.Exp, scale=SCALE)
                        if mslot is not None:
                            mc = (mslot - s0) * 128
                            k.op("dve", "tensor_tensor", reads=[ptag, "mskT"], writes=[ptag], out=pt[:, mc:mc + 128], in0=pt[:, mc:mc + 128],
                                 in1=mskT[:, (t_ - 1) % 4, :], op=ALU.mult)
                        return (ni, t_, nk, cl, pt, ptag)

                    def stage_b(st_):
                        ni, t_, nk, cl, pt, ptag = st_
                        xt_ = t_ - 1
                        closing = (t_ > 0 and xt_ % 4 == 3 and xt_ // 4 >= s0)
                        parts = [(cl, cl + 128, True), (cl + 128, 512, False)] if (closing and cl + 128 < 512) else [(cl, 512, closing)]
                        for (ca, cb_, stp) in parts:
                            k.op("pe", "matmul", reads=["Vsb", ptag], writes=["pb2"], out=pb[2][:, ca:cb_], lhsT=Vsb[0:nk, t_, :], rhs=pt[0:nk, ca:cb_],
                                 start=(ni == 0), stop=stp)
                            k.op("pe", "matmul", reads=["ones_b", ptag], writes=["pb3"], out=pb[3][:, ca:cb_], lhsT=ones_b[0:nk, :], rhs=pt[0:nk, ca:cb_],
                                 start=(ni == 0), stop=stp)

                    prev_ = None
                    for ni, t_ in enumerate(need):
                        cur_ = stage_a(ni, t_)
                        if prev_ is not None:
                            stage_b(prev_)
                        prev_ = cur_
                    stage_b(prev_)
                    k.op("dve", "reciprocal", reads=["pb3"], writes=["rden"], out=rden[:, :], in_=pb[3][:, :])
                    k.op("dve", "tensor_tensor", reads=["pb2", "rden"], writes=["attnT"], out=attnT[:, h, qb * 512:(qb + 1) * 512], in0=pb[2][:, :],
                         in1=rden[:, :], op=ALU.mult)
            if stage == 4:
                k.final_tokens.append(k.dma("sp", dbg["attn"].rearrange("p (h c) -> p h c", h=16), attnT[:], reads=["attnT"], writes=["dbg_attn"]))
            arena.release(m_att)
            k.barrier()
        if stage >= 5:
            r1 = Arena(arena.ap[:, r1_lo:r1_hi], r1_hi - r1_lo)
            mixedT = r1.alloc("mixedT", [128, KT, NOWN], BF16)
            m_mg = arena.mark()
            wm = [dict(ap=sb(f"wm_ap{i}", [128, 16, 128], BF16), g0=sb(f"wm_g0{i}", [128, 16, 128], BF16), g1=sb(f"wm_g1{i}", [128, 16, 128], BF16),
                       gv=sb(f"wm_gv{i}", [128, 8, 128], BF16), gg=sb(f"wm_gg{i}", [128, 8, 128], BF16)) for i in range(2)]
            sg = [sb(f"sg{i}", [128, 512]) for i in range(3)]; mt1 = sb("mt1", [128, 512]); mt2 = sb("mt2", [128, 512])
            for mt in range(16):
                ws = mt % 2
                W = wm[ws]
                for nm, src_ in [("ap", wap_d[mt]), ("g0", wown[12 + mt]), ("g1", wown[28 + mt]), ("gv", wgv_d[mt]), ("gg", wgg_d[mt])]:
                    k.dma("pool", W[nm][:], src_, writes=[f"wm_{nm}{ws}"])
                for tb_ in range(2):
                    tc_ = slice(tb_ * 512, (tb_ + 1) * 512)
                    for bi, (nm, act_, nk_, atag) in enumerate([("ap", attnT, 16, "attnT"), ("g0", xoT, 16, "xoT"), ("g1", xoT, 16, "xoT"),
                                                                ("gv", gT, 8, "gT"), ("gg", gT, 8, "gT")]):
                        for kt in range(nk_):
                            k.op("pe", "matmul", reads=[f"wm_{nm}{ws}", atag], writes=[f"pb{bi}"], out=pb[bi][:, :], lhsT=W[nm][:, kt, :],
                                 rhs=act_[:, kt, tc_], start=(kt == 0), stop=(kt == nk_ - 1))
                    for si, bi in enumerate((1, 2, 4)):
                        k.op("act", "activation", reads=[f"pb{bi}"], writes=[f"sg{si}"], out=sg[si][:, :], in_=pb[bi][:, :], func=AF.Sigmoid)
                    k.op("dve", "tensor_tensor", reads=["pb3", "sg2"], writes=["mt1"], out=mt1[:, :], in0=pb[3][:, :], in1=sg[2][:, :], op=ALU.mult)
                    k.op("dve", "tensor_tensor", reads=["mt1", "sg1"], writes=["mt1"], out=mt1[:, :], in0=mt1[:, :], in1=sg[1][:, :], op=ALU.mult)
                    k.op("dve", "tensor_tensor", reads=["pb0", "sg0"], writes=["mt2"], out=mt2[:, :], in0=pb[0][:, :], in1=sg[0][:, :], op=ALU.mult)
                    k.op("dve", "tensor_tensor", reads=["mt1", "mt2"], writes=["mixedT"], out=mixedT[:, mt, tc_], in0=mt1[:, :], in1=mt2[:, :], op=ALU.add)
            if stage == 5:
                k.final_tokens.append(k.dma("sp", dbg["mixed"].rearrange("p (m c) -> p m c", m=16), mixedT[:], reads=["mixedT"], writes=["dbg_mixed"]))
            arena.release(m_mg)
            k.barrier()

        if stage >= 6:
            arena.release(M0)
            k.barrier()
            h1 = sb("h1", [128, 8, D])
            for tt in range(8):
                k.dma("sp", h1[:, tt, :], xo[tt * 128:(tt + 1) * 128, :], writes=[f"h1_{tt}"])
            m_wo = arena.mark()
            wo = [sb(f"wo{i}", [128, KT, 512], BF16) for i in range(2)]
            for cb in range(4):
                ws = cb % 2
                k.dma("pool", wo[ws][:], wout_d[cb], writes=[f"wo{ws}"])
                for tt in range(8):
                    B = 4 + tt % 2
                    for kt in range(KT):
                        k.op("pe", "matmul", reads=[f"wo{ws}", "mixedT"], writes=[f"pb{B}"], out=pb[B][:, :], lhsT=mixedT[:, kt, tt * 128:(tt + 1) * 128],
                             rhs=wo[ws][:, kt, :], start=(kt == 0), stop=(kt == KT - 1))
                    k.op("dve", "tensor_tensor", reads=[f"pb{B}", f"h1_{tt}"], writes=[f"h1_{tt}"], out=h1[:, tt, cb * 512:(cb + 1) * 512],
                         in0=pb[B][:, :], in1=h1[:, tt, cb * 512:(cb + 1) * 512], op=ALU.add)
            arena.release(m_wo)
            k.barrier()
            r2.release(0)
            mT = r2.alloc("mT", [128, KT, NOWN], BF16)
            gffnT = sb("gffnT_s", [128, KT]); k.dma("sp", gffnT[:, :], gffnT_d, writes=["gffnT"])
            m_f = arena.mark()
            xs = sb("xs", [128, D], BF16); junk = xs; ss = sb("ss", [128, 1]); rs = sb("rs", [128, 1])
            for tt in range(8):
                norm_transpose(None, 128, 0, mT, tt * 128, gffnT, "mT", sb_src=(h1[:, tt, :], f"h1_{tt}"), gtag="gffnT")
            r1.release(0)
            aT = r1.alloc("aT", [128, 11, NOWN], BF16)
            wdb = [r1.alloc("wd0", [128, 11, 512], BF16), sb("wd1", [128, 11, 512], BF16)]
            wgu = [dict(g=sb(f"wfg{i}", [128, KT, 128], BF16), u=sb(f"wfu{i}", [128, KT, 128], BF16)) for i in range(3)]
            sl = sb("sl", [128, 512])
            wdi = [0]
            for fg in range(4):
                for mi in range(11):
                    mt = fg * 11 + mi
                    ws = mt % 3
                    k.dma("pool", wgu[ws]["g"][:], wfg_d[mt], writes=[f"wfg{ws}"]); k.dma("pool", wgu[ws]["u"][:], wfu_d[mt], writes=[f"wfu{ws}"])
                    for tb_ in range(2):
                        tc_ = slice(tb_ * 512, (tb_ + 1) * 512)
                        Bg, Bu = 2 * tb_, 2 * tb_ + 1
                        for nm, B in (("g", Bg), ("u", Bu)):
                            for kt in range(KT):
                                k.op("pe", "matmul", reads=[f"wf{nm}{ws}", "mT"], writes=[f"pb{B}"], out=pb[B][:, :], lhsT=wgu[ws][nm][:, kt, :],
                                     rhs=mT[:, kt, tc_], start=(kt == 0), stop=(kt == KT - 1))
                        k.op("act", "activation", reads=[f"pb{Bg}"], writes=["sl"], out=sl[:, :], in_=pb[Bg][:, :], func=AF.Silu)
                        k.op("dve", "tensor_tensor", reads=["sl", f"pb{Bu}"], writes=["aT"], out=aT[:, mi, tc_], in0=sl[:, :], in1=pb[Bu][:, :], op=ALU.mult)
                for cb in range(4):
                    ws = wdi[0] % 2; wdi[0] += 1
                    k.dma("pool", wdb[ws][:], wfd_d[fg, cb], writes=[f"wd{ws}"])
                    for tt in range(8):
                        B = 4 + tt % 2
                        for mi in range(11):
                            k.op("pe", "matmul", reads=[f"wd{ws}", "aT"], writes=[f"pb{B}"], out=pb[B][:, :], lhsT=aT[:, mi, tt * 128:(tt + 1) * 128],
                                 rhs=wdb[ws][:, mi, :], start=(mi == 0), stop=(mi == 10))
                        k.op("dve", "tensor_tensor", reads=[f"pb{B}", f"h1_{tt}"], writes=[f"h1_{tt}"], out=h1[:, tt, cb * 512:(cb + 1) * 512],
                             in0=pb[B][:, :], in1=h1[:, tt, cb * 512:(cb + 1) * 512], op=ALU.add)
            arena.release(m_f)
            k.barrier()
            gfin = sb("gfin_s", [128, D]); k.dma("sp", gfin[:, :], gfin_d, writes=["gfin"])
            ob = [sb(f"ob{i}", [128, D]) for i in range(2)]; jf = sb("jf", [128, D], BF16); ssf = sb("ssf", [128, 1]); rsf = sb("rsf", [128, 1])
            for tt in range(8):
                o_ = ob[tt % 2]; otag = f"ob{tt % 2}"
                k.op("act", "activation", reads=[f"h1_{tt}"], writes=["jf", "ssf"], out=jf[:, :], in_=h1[:, tt, :], func=AF.Square, accum_out=ssf[:, :])
                k.op("act", "activation", reads=["ssf"], writes=["rsf"], out=rsf[:, :], in_=ssf[:, :], func=AF.Sqrt, scale=1.0 / D, bias=EPS)
                k.op("dve", "reciprocal", reads=["rsf"], writes=["rsf"], out=rsf[:, :], in_=rsf[:, :])
                k.op("dve", "scalar_tensor_tensor", reads=[f"h1_{tt}", "rsf", "gfin"], writes=[otag], out=o_[:, :], in0=h1[:, tt, :], scalar=rsf[:, 0:1],
                     in1=gfin[:, :], op0=ALU.mult, op1=ALU.mult)
                k.final_tokens.append(k.dma("sp", out_d[tt * 128:(tt + 1) * 128, :], o_[:, :], reads=[otag], writes=[f"out{tt}"]))
        print(f"[build] arena peak = {arena.peak * 4} B/partition of {AW * 4}; instr counts = {k.cnt}; sems = {k.nsem}")

        with nc.Block() as block:
            k.emit(block)
    return nc


def kernel(**inputs):
    maps = host_layout(inputs)
    nc = build()
    res = run_bass_kernel_spmd(nc, maps, core_ids=list(range(NCORES)))
    out = np.empty((2, SEQ, D), np.float32)
    for core in range(NCORES):
        b, c = divmod(core, 4)
        out[b].reshape(8, 4, 128, D)[:, c] = np.asarray(res.results[core]["out"], np.float32).reshape(8, 128, D)
    return out
```
